# Optimizing a Trainium2 kernel written in Bass

```python
import math
import jax
import jax.numpy as jnp
from jax import lax

D_MODEL = 1024
BATCH = 8
SEQ = 4096
DEPTH = 2

MEM_LEN = 256
EPS = 1e-6
NEG_INF = -1e30
FORCE_SCORE = 1e4

POOL_WIDTH = D_MODEL // 2
POOL_WINDOWS = (2, 4, 8, 16)
POOL_GROUPS = len(POOL_WINDOWS)
POOL_GROUP_DIM = POOL_WIDTH // POOL_GROUPS
DN_WIDTH = D_MODEL - POOL_WIDTH
DN_HEAD_DIM = 128
DN_HEADS = DN_WIDTH // DN_HEAD_DIM
DN_CONV = 4
DN_CHUNK = 64
A_SIZES = (POOL_WIDTH, DN_WIDTH, DN_WIDTH, DN_WIDTH, DN_WIDTH, DN_HEADS, DN_HEADS)
IN_A_COLS = POOL_WIDTH + 4 * DN_WIDTH + 2 * DN_HEADS

NSA_HEAD_DIM = 64
NSA_HEADS = D_MODEL // NSA_HEAD_DIM
NSA_GROUPS = 4
NSA_REP = NSA_HEADS // NSA_GROUPS
NSA_KV = NSA_GROUPS * NSA_HEAD_DIM
CMP_LEN = 32
CMP_STRIDE = 16
CMP_HIDDEN = 2 * NSA_HEAD_DIM
SLC_LEN = 64
SLC_TOP = 16
WINDOW = 512
NSA_QBLOCK = 32
C_SIZES = (D_MODEL, NSA_KV, NSA_KV, NSA_KV, NSA_KV, NSA_KV, NSA_KV, 3 * NSA_HEADS)
IN_C_COLS = D_MODEL + 6 * NSA_KV + 3 * NSA_HEADS

XA_HEADS = 4
XA_HEAD_DIM = D_MODEL // XA_HEADS
FF_DIM = 4 * D_MODEL

N_EVEN = (DEPTH + 1) // 2
N_ODD = DEPTH // 2

kernel_name = 'hybrid_pool_deltanet_nsa_trunk'


def rmsnorm(x, g):
    xf = x.astype(jnp.float32)
    y = xf * lax.rsqrt(jnp.mean(xf * xf, axis=-1, keepdims=True) + EPS)
    return (y * g.astype(jnp.float32)).astype(x.dtype)


def l2norm(x):
    return x * lax.rsqrt(jnp.sum(x * x, axis=-1, keepdims=True) + EPS)


def alibi_slopes(n_heads):
    h = jnp.arange(1, n_heads + 1, dtype=jnp.float32)
    return jnp.exp2(-8.0 * h / n_heads)


def split_cols(z, sizes):
    outs, start = [], 0
    for n in sizes:
        outs.append(z[..., start:start + n])
        start += n
    return outs


def causal_depthwise_conv(x, w):
    width, ch = w.shape
    return lax.conv_general_dilated(
        x, w[:, None, :].astype(x.dtype), window_strides=(1,), padding=[(width - 1, 0)],
        dimension_numbers=('NWC', 'WIO', 'NWC'), feature_group_count=ch)


def multiscale_pool(u, pool_w, pool_scale):
    b, s, _ = u.shape
    uf = u.astype(jnp.float32)
    pos1 = jnp.arange(1, s + 1, dtype=jnp.float32)[None, :, None]
    outs = []
    for gi, win in enumerate(POOL_WINDOWS):
        ug = uf[..., gi * POOL_GROUP_DIM:(gi + 1) * POOL_GROUP_DIM]
        csum = jnp.pad(jnp.cumsum(ug, axis=1), ((0, 0), (1, 0), (0, 0)))
        lower = jnp.pad(csum, ((0, 0), (win - 1, 0), (0, 0)))[:, :s]
        mean = (csum[:, 1:] - lower) / jnp.minimum(pos1, float(win))
        outs.append(mean - ug)
    y = jnp.stack(outs, axis=2)
    y = jnp.einsum('bsgc,gcd->bsgd', y, pool_w.astype(jnp.float32)).reshape(b, s, POOL_WIDTH)
    return (y * pool_scale.astype(jnp.float32)).astype(u.dtype)


def gated_delta_rule(q, k, v, beta, log_decay):
    b, h, s, dk = q.shape
    dv = v.shape[-1]
    c = DN_CHUNK
    n = s // c
    q = q * (dk ** -0.5)
    q, k, v = (a.reshape(b, h, n, c, a.shape[-1]) for a in (q, k, v))
    beta = beta.reshape(b, h, n, c)
    gc = jnp.cumsum(log_decay.reshape(b, h, n, c), axis=-1)
    causal = jnp.tril(jnp.ones((c, c), dtype=bool))
    strict = jnp.tril(jnp.ones((c, c), dtype=bool), -1)
    diff = gc[..., :, None] - gc[..., None, :]
    decay_mat = jnp.where(causal, jnp.exp(jnp.where(causal, diff, 0.0)), 0.0)
    kb = k * beta[..., None]
    a_low = jnp.where(strict, jnp.einsum('bhnid,bhnjd->bhnij', kb, k) * decay_mat, 0.0)
    eye = jnp.eye(c, dtype=jnp.float32)
    t_mat = lax.linalg.triangular_solve(eye + a_low, jnp.broadcast_to(eye, a_low.shape),
                                        left_side=True, lower=True, unit_diagonal=True)
    w_c = jnp.matmul(t_mat, kb * jnp.exp(gc)[..., None])
    u_c = jnp.matmul(t_mat, v * beta[..., None])
    attn = jnp.where(causal, jnp.einsum('bhnid,bhnjd->bhnij', q, k) * decay_mat, 0.0)

    def step(state, xs):
        q_i, k_i, w_i, u_i, attn_i, gc_i = xs
        v_new = u_i - jnp.matmul(w_i, state)
        o_i = jnp.matmul(q_i * jnp.exp(gc_i)[..., None], state) + jnp.matmul(attn_i, v_new)
        g_last = gc_i[..., -1]
        k_dec = k_i * jnp.exp(g_last[..., None] - gc_i)[..., None]
        state = state * jnp.exp(g_last)[..., None, None] + jnp.einsum('bhcd,bhce->bhde', k_dec, v_new)
        return state, o_i

    xs = tuple(jnp.moveaxis(a, 2, 0) for a in (q, k, w_c, u_c, attn, gc))
    state0 = jnp.zeros((b, h, dk, dv), jnp.float32)
    _, o = lax.scan(step, state0, xs)
    return jnp.moveaxis(o, 0, 2).reshape(b, h, s, dv)


def gated_deltanet(q, k, v, gate, beta_logit, alpha_logit, conv_w, a_log, dt_bias, o_norm):
    b, s, _ = q.shape
    f32 = jnp.float32
    qkv = jax.nn.silu(causal_depthwise_conv(jnp.concatenate([q, k, v], axis=-1), conv_w))
    qkv = jnp.transpose(qkv.astype(f32).reshape(b, s, 3, DN_HEADS, DN_HEAD_DIM), (2, 0, 3, 1, 4))
    qh, kh, vh = l2norm(qkv[0]), l2norm(qkv[1]), qkv[2]
    beta = jnp.transpose(jax.nn.sigmoid(beta_logit.astype(f32)), (0, 2, 1))
    log_decay = -jnp.exp(a_log.astype(f32)) * jax.nn.softplus(alpha_logit.astype(f32) + dt_bias.astype(f32))
    o = gated_delta_rule(qh, kh, vh, beta, jnp.transpose(log_decay, (0, 2, 1)))
    o = jnp.transpose(o, (0, 2, 1, 3))
    o = rmsnorm(o, o_norm) * jax.nn.silu(gate.astype(f32).reshape(b, s, DN_HEADS, DN_HEAD_DIM))
    return o.reshape(b, s, DN_WIDTH).astype(q.dtype)


def pool_delta_mixer(h, w_in, pool_w, pool_scale, conv_w, a_log, dt_bias, o_norm, w_out):
    z = h @ w_in
    u, q, k, v, gate, beta_l, alpha_l = split_cols(z, A_SIZES)
    y_pool = multiscale_pool(u, pool_w, pool_scale)
    y_dn = gated_deltanet(q, k, v, gate, beta_l, alpha_l, conv_w, a_log, dt_bias, o_norm)
    return jnp.concatenate([y_pool, y_dn], axis=-1) @ w_out


def compress_kv(kv, pe, w1, w2):
    b, s, g, d = kv.shape
    n_str = s // CMP_STRIDE
    per = CMP_LEN // CMP_STRIDE
    n_cmp = n_str - per + 1
    c = kv.reshape(b, n_str, CMP_STRIDE, g, d)
    blocks = jnp.concatenate([c[:, j:j + n_cmp] for j in range(per)], axis=2)
    blocks = blocks + pe[None, None, :, None, :].astype(kv.dtype)
    flat = jnp.transpose(blocks, (0, 1, 3, 2, 4)).reshape(b, n_cmp, g, CMP_LEN * d)
    return jax.nn.silu(flat @ w1) @ w2


def nsa_attention(q, kc, vc, ks, vs, kw, vw, gate_logit, pe_k, w1_k, w2_k, pe_v, w1_v, w2_v):
    b, s, _ = q.shape
    dt = q.dtype
    f32 = jnp.float32
    g_, r_, d_ = NSA_GROUPS, NSA_REP, NSA_HEAD_DIM
    qh = q.reshape(b, s, g_, r_, d_)
    ck = compress_kv(kc.reshape(b, s, g_, d_), pe_k, w1_k, w2_k)
    cv = compress_kv(vc.reshape(b, s, g_, d_), pe_v, w1_v, w2_v)
    n_cmp = ck.shape[1]
    n_slc = s // SLC_LEN
    top_n = min(SLC_TOP, n_slc)
    ks_b = jnp.transpose(ks.reshape(b, n_slc, SLC_LEN, g_, d_), (0, 3, 1, 2, 4))
    vs_b = jnp.transpose(vs.reshape(b, n_slc, SLC_LEN, g_, d_), (0, 3, 1, 2, 4))
    kw_p = jnp.pad(kw.reshape(b, s, g_, d_), ((0, 0), (WINDOW, 0), (0, 0), (0, 0)))
    vw_p = jnp.pad(vw.reshape(b, s, g_, d_), ((0, 0), (WINDOW, 0), (0, 0), (0, 0)))
    gates = jax.nn.sigmoid(gate_logit.astype(f32)).reshape(b, s, g_, r_, 3)
    slopes = alibi_slopes(NSA_HEADS).reshape(g_, r_)
    scale = d_ ** -0.5
    c_lo = jnp.arange(n_cmp) * CMP_STRIDE
    cmp_end = c_lo + CMP_LEN - 1
    s_lo = jnp.arange(n_slc) * SLC_LEN
    overlap = jnp.clip(jnp.minimum(c_lo[:, None] + CMP_LEN, s_lo[None, :] + SLC_LEN)
                       - jnp.maximum(c_lo[:, None], s_lo[None, :]), 0, None).astype(f32) / CMP_LEN
    blk = jnp.arange(n_slc)
    in_blk = jnp.arange(SLC_LEN)
    win_off = jnp.arange(WINDOW + NSA_QBLOCK)
    b_idx = jnp.arange(b)[:, None, None, None]
    g_idx = jnp.arange(g_)[None, :, None, None]

    def one_block(i):
        q0 = i * NSA_QBLOCK
        qb = lax.dynamic_slice_in_dim(qh, q0, NSA_QBLOCK, axis=1)
        t = q0 + jnp.arange(NSA_QBLOCK)
        dist_c = (t[:, None] - cmp_end[None, :]).astype(f32)
        ok_c = dist_c >= 0
        sc = jnp.einsum('bqgrd,bcgd->bgrqc', qb, ck).astype(f32) * scale
        sc = jnp.where(ok_c, sc - slopes[:, :, None, None] * dist_c, NEG_INF)
        p_cmp = jnp.where(jnp.any(ok_c, axis=-1)[:, None], jax.nn.softmax(sc, axis=-1), 0.0)
        o_cmp = jnp.einsum('bgrqc,bcgd->bqgrd', p_cmp.astype(dt), cv)
        imp = jnp.einsum('bgrqc,cn->bgqn', p_cmp, overlap)
        cur = t[:, None] // SLC_LEN
        forced = (blk[None, :] == 0) | (blk[None, :] == cur) | (blk[None, :] == cur - 1)
        causal_blk = blk[None, :] * SLC_LEN <= t[:, None]
        imp = jnp.where(forced, FORCE_SCORE, jnp.where(causal_blk, imp, -1.0))
        _, idx = lax.top_k(imp, top_n)
        kg = ks_b[b_idx, g_idx, idx]
        vg = vs_b[b_idx, g_idx, idx]
        dist_s = (t[None, None, :, None, None] - (idx[..., None] * SLC_LEN + in_blk)).astype(f32)
        ss = jnp.einsum('bqgrd,bgqnld->bgrqnl', qb, kg).astype(f32) * scale
        ss = jnp.where(dist_s[:, :, None] >= 0,
                       ss - slopes[None, :, :, None, None, None] * dist_s[:, :, None], NEG_INF)
        p_slc = jax.nn.softmax(ss.reshape(b, g_, r_, NSA_QBLOCK, -1), axis=-1).reshape(ss.shape)
        o_slc = jnp.einsum('bgrqnl,bgqnld->bqgrd', p_slc.astype(dt), vg)
        kwb = lax.dynamic_slice_in_dim(kw_p, q0, WINDOW + NSA_QBLOCK, axis=1)
        vwb = lax.dynamic_slice_in_dim(vw_p, q0, WINDOW + NSA_QBLOCK, axis=1)
        kpos = q0 - WINDOW + win_off
        dist_w = t[:, None] - kpos[None, :]
        ok_w = (dist_w >= 0) & (dist_w < WINDOW) & (kpos[None, :] >= 0)
        sw = jnp.einsum('bqgrd,bkgd->bgrqk', qb, kwb).astype(f32) * scale
        sw = jnp.where(ok_w, sw - slopes[:, :, None, None] * dist_w.astype(f32), NEG_INF)
        p_win = jax.nn.softmax(sw, axis=-1)
        o_win = jnp.einsum('bgrqk,bkgd->bqgrd', p_win.astype(dt), vwb)
        gb = lax.dynamic_slice_in_dim(gates, q0, NSA_QBLOCK, axis=1)
        o = gb[..., 0:1] * o_cmp + gb[..., 1:2] * o_slc + gb[..., 2:3] * o_win
        return o.reshape(b, NSA_QBLOCK, NSA_HEADS * d_).astype(dt)

    out = lax.map(one_block, jnp.arange(s // NSA_QBLOCK))
    return jnp.moveaxis(out, 0, 1).reshape(b, s, NSA_HEADS * d_)


def nsa_mixer(h, w_in, pe_k, w1_k, w2_k, pe_v, w1_v, w2_v, w_out):
    z = h @ w_in
    q, kc, vc, ks, vs, kw, vw, gl = split_cols(z, C_SIZES)
    return nsa_attention(q, kc, vc, ks, vs, kw, vw, gl, pe_k, w1_k, w2_k, pe_v, w1_v, w2_v) @ w_out


def memory_xattn(h, mem_h, wq, wk, wv, wo):
    b, s, _ = h.shape
    m = mem_h.shape[1]
    q = (h @ wq).reshape(b, s, XA_HEADS, XA_HEAD_DIM)
    k = (mem_h @ wk).reshape(b, m, XA_HEADS, XA_HEAD_DIM)
    v = (mem_h @ wv).reshape(b, m, XA_HEADS, XA_HEAD_DIM)
    sc = jnp.einsum('bshd,bmhd->bhsm', q, k).astype(jnp.float32) * (XA_HEAD_DIM ** -0.5)
    p = jax.nn.softmax(sc, axis=-1).astype(h.dtype)
    o = jnp.einsum('bhsm,bmhd->bshd', p, v).reshape(b, s, D_MODEL)
    return o @ wo


def squared_relu_mlp(h, w1, w2):
    return jnp.square(jax.nn.relu(h @ w1)) @ w2


def setup_inputs(seed: int = 0) -> dict:
    key = jax.random.key(seed)
    keys = iter(jax.random.split(key, 40))
    f32 = jnp.float32

    def dense(shape, fan_in):
        return jax.random.normal(next(keys), shape, f32) * (fan_in ** -0.5)

    def gain(shape):
        return 1.0 + 0.05 * jax.random.normal(next(keys), shape, f32)

    ne, no, dep = N_EVEN, N_ODD, DEPTH
    x = jax.random.normal(next(keys), (BATCH, SEQ, D_MODEL), f32)
    mem = jax.random.normal(next(keys), (BATCH, MEM_LEN, D_MODEL), f32)
    a_ln = gain((ne, D_MODEL))
    a_w_in = dense((ne, D_MODEL, IN_A_COLS), D_MODEL)
    a_pool_w = dense((ne, POOL_GROUPS, POOL_GROUP_DIM, POOL_GROUP_DIM), POOL_GROUP_DIM)
    a_pool_scale = gain((ne, POOL_WIDTH))
    a_conv_w = dense((ne, DN_CONV, 3 * DN_WIDTH), DN_CONV)
    a_a_log = jnp.log(jax.random.uniform(next(keys), (ne, DN_HEADS), f32, 1.0, 16.0))
    dt_init = jnp.exp(jax.random.uniform(next(keys), (ne, DN_HEADS), f32, math.log(1e-3), math.log(1e-1)))
    a_dt_bias = dt_init + jnp.log(-jnp.expm1(-dt_init))
    a_o_norm = gain((ne, DN_HEAD_DIM))
    a_w_out = dense((ne, D_MODEL, D_MODEL), D_MODEL)
    c_ln = gain((no, D_MODEL))
    c_w_in = dense((no, D_MODEL, IN_C_COLS), D_MODEL)
    c_pe_k = 0.1 * jax.random.normal(next(keys), (no, CMP_LEN, NSA_HEAD_DIM), f32)
    c_w1_k = dense((no, CMP_LEN * NSA_HEAD_DIM, CMP_HIDDEN), CMP_LEN * NSA_HEAD_DIM)
    c_w2_k = dense((no, CMP_HIDDEN, NSA_HEAD_DIM), CMP_HIDDEN)
    c_pe_v = 0.1 * jax.random.normal(next(keys), (no, CMP_LEN, NSA_HEAD_DIM), f32)
    c_w1_v = dense((no, CMP_LEN * NSA_HEAD_DIM, CMP_HIDDEN), CMP_LEN * NSA_HEAD_DIM)
    c_w2_v = dense((no, CMP_HIDDEN, NSA_HEAD_DIM), CMP_HIDDEN)
    c_w_out = dense((no, D_MODEL, D_MODEL), D_MODEL)
    xa_ln = gain((dep, D_MODEL))
    xa_mem_ln = gain((dep, D_MODEL))
    xa_wq = dense((dep, D_MODEL, D_MODEL), D_MODEL)
    xa_wk = dense((dep, D_MODEL, D_MODEL), D_MODEL)
    xa_wv = dense((dep, D_MODEL, D_MODEL), D_MODEL)
    xa_wo = dense((dep, D_MODEL, D_MODEL), D_MODEL)
    ff_ln = gain((dep, D_MODEL))
    ff_w1 = dense((dep, D_MODEL, FF_DIM), D_MODEL)
    ff_w2 = dense((dep, FF_DIM, D_MODEL), FF_DIM)
    final_ln = gain((D_MODEL,))
    return {'x': x, 'mem': mem,
            'a_ln': a_ln, 'a_w_in': a_w_in, 'a_pool_w': a_pool_w, 'a_pool_scale': a_pool_scale,
            'a_conv_w': a_conv_w, 'a_a_log': a_a_log, 'a_dt_bias': a_dt_bias, 'a_o_norm': a_o_norm,
            'a_w_out': a_w_out,
            'c_ln': c_ln, 'c_w_in': c_w_in, 'c_pe_k': c_pe_k, 'c_w1_k': c_w1_k, 'c_w2_k': c_w2_k,
            'c_pe_v': c_pe_v, 'c_w1_v': c_w1_v, 'c_w2_v': c_w2_v, 'c_w_out': c_w_out,
            'xa_ln': xa_ln, 'xa_mem_ln': xa_mem_ln, 'xa_wq': xa_wq, 'xa_wk': xa_wk, 'xa_wv': xa_wv,
            'xa_wo': xa_wo,
            'ff_ln': ff_ln, 'ff_w1': ff_w1, 'ff_w2': ff_w2, 'final_ln': final_ln}


def reference(x, mem,
              a_ln, a_w_in, a_pool_w, a_pool_scale, a_conv_w, a_a_log, a_dt_bias, a_o_norm, a_w_out,
              c_ln, c_w_in, c_pe_k, c_w1_k, c_w2_k, c_pe_v, c_w1_v, c_w2_v, c_w_out,
              xa_ln, xa_mem_ln, xa_wq, xa_wk, xa_wv, xa_wo,
              ff_ln, ff_w1, ff_w2, final_ln):
    for l in range(DEPTH):
        i = l // 2
        if l % 2 == 0:
            x = x + pool_delta_mixer(rmsnorm(x, a_ln[i]), a_w_in[i], a_pool_w[i], a_pool_scale[i],
                                     a_conv_w[i], a_a_log[i], a_dt_bias[i], a_o_norm[i], a_w_out[i])
        else:
            x = x + nsa_mixer(rmsnorm(x, c_ln[i]), c_w_in[i], c_pe_k[i], c_w1_k[i], c_w2_k[i],
                              c_pe_v[i], c_w1_v[i], c_w2_v[i], c_w_out[i])
        x = x + memory_xattn(rmsnorm(x, xa_ln[l]), rmsnorm(mem, xa_mem_ln[l]),
                             xa_wq[l], xa_wk[l], xa_wv[l], xa_wo[l])
        x = x + squared_relu_mlp(rmsnorm(x, ff_ln[l]), ff_w1[l], ff_w2[l])
    return rmsnorm(x, final_ln)
```

```python
import contextlib
import numpy as np
import ml_dtypes
import concourse.bass as bass
import concourse.mybir as mybir
from concourse.bass_utils import run_bass_kernel_spmd

F32 = mybir.dt.float32
BF16 = mybir.dt.bfloat16
ALU = mybir.AluOpType
AF = mybir.ActivationFunctionType
AX = mybir.AxisListType

S = 4096
D = 1024
NT = S // 128
N_DMA_SEMS = 6
BIG = 30000.0
SYNC_ALL = False


def _region(ap):
    t = ap.tensor
    shape = tuple(t.shape)
    space = str(ap.space)
    off = int(ap.offset)
    pat = ap.ap
    if space in ("SB", "PSUM"):
        pitch = 1
        for s in shape[1:]:
            pitch *= int(s)
        p0 = off // pitch
        f0 = off % pitch
        npart = int(pat[0][1])
        if space == "PSUM":
            return (ap.name, (p0 // 32) * 32, ((p0 + npart + 31) // 32) * 32, 0, pitch)
        ext = 1
        for st, cnt in pat[1:]:
            ext += (int(cnt) - 1) * abs(int(st))
        return (ap.name, p0, p0 + npart, f0, f0 + ext)
    ext = 1
    for st, cnt in pat:
        ext += (int(cnt) - 1) * abs(int(st))
    return (ap.name, 0, 1, off, off + ext)


def _overlap(a, b):
    return a[1] < b[2] and b[1] < a[2] and a[3] < b[4] and b[3] < a[4]


def _contains(a, b):
    return a[1] <= b[1] and a[2] >= b[2] and a[3] <= b[3] and a[4] >= b[4]


class Ins:
    __slots__ = ("idx", "eng", "emit", "dma", "deps", "signal", "sig", "waits", "inc", "pseudo")

    def __init__(self, idx, eng, emit, dma):
        self.idx = idx
        self.eng = eng
        self.emit = emit
        self.dma = dma
        self.deps = set()
        self.signal = dma
        self.sig = None
        self.waits = []
        self.inc = 1
        self.pseudo = False


class Prog:
    ENGS = ["pe", "act", "dve", "pool", "sp"]

    def __init__(self, nc):
        self.nc = nc
        self.ins = []
        self.rec = {}
        self.dma_rr = {}
        self.dma_last = {}

    def add(self, eng, emit, reads, writes, dma=False):
        i = Ins(len(self.ins), eng, emit, dma)
        self.ins.append(i)
        reads = [a for a in reads if a is not None and not isinstance(a, (int, float))]
        writes = [a for a in writes if a is not None]
        psr = [a for a in reads if str(a.space) == "PSUM"]
        reads = [a for a in reads if str(a.space) != "PSUM"]
        writes = list(writes) + psr
        for ap in reads:
            r = _region(ap)
            lst = self.rec.get(r[0], [])
            keep = []
            for rec in lst:
                reg, j, w = rec
                if w and _overlap(reg, r):
                    self._dep(i, j, True)
                if (not w) and (not dma) and self.ins[j].eng == eng and (not self.ins[j].dma) and _contains(r, reg):
                    continue
                keep.append(rec)
            keep.append((r, i.idx, False))
            self.rec[r[0]] = keep
        for ap in writes:
            if ap is None:
                continue
            r = _region(ap)
            lst = self.rec.get(r[0], [])
            keep = []
            for rec in lst:
                reg, j, w = rec
                if j == i.idx:
                    keep.append(rec)
                    continue
                if _overlap(reg, r):
                    self._dep(i, j, False)
                    if _contains(r, reg):
                        continue
                keep.append(rec)
            keep.append((r, i.idx, True))
            self.rec[r[0]] = keep
        if dma:
            k = self.dma_rr.get(eng, 0)
            self.dma_rr[eng] = (k + 1) % N_DMA_SEMS
            key = (eng, k)
            if key in self.dma_last:
                self._dep(i, self.dma_last[key], True)
            self.dma_last[key] = i.idx
            i.sig = key
        return i

    def barrier(self):
        last_eng, last_dma = {}, {}
        for i in self.ins:
            if i.dma:
                last_dma[i.sig if not isinstance(i.sig[0], tuple) else i.sig[0]] = i.idx
            elif not i.pseudo:
                last_eng[i.eng] = i.idx
        targets = set(last_eng.values()) | set(last_dma.values())
        for e in self.ENGS:
            b = self.add(e, lambda e_: None, [], [])
            b.pseudo = True
            for j in targets:
                if j != b.idx:
                    b.deps.add(j)
                    self.ins[j].signal = True

    def _dep(self, i, j, raw):
        if j == i.idx:
            return
        pj = self.ins[j]
        if (not pj.dma) and (not i.dma) and pj.eng == i.eng:
            if i.eng == "pe" or (not raw and not SYNC_ALL):
                return
        i.deps.add(j)
        pj.signal = True

    def mm(self, out, lhsT, rhs, start=True, stop=True, **kw):
        return self.add("pe", lambda e: e.matmul(out, lhsT, rhs, start=start, stop=stop, **kw), [lhsT, rhs], [out])

    def transpose(self, out, in_, ident):
        return self.add("pe", lambda e: e.transpose(out, in_, ident), [in_, ident], [out])

    def act(self, out, in_, func, bias=None, scale=None, accum_out=None):
        kw = {}
        rd = [in_]
        if bias is not None:
            kw["bias"] = bias
            rd.append(bias)
        if scale is not None:
            kw["scale"] = scale
            rd.append(scale)
        wr = [out]
        if accum_out is not None:
            kw["accum_out"] = accum_out
            wr.append(accum_out)
        return self.add("act", lambda e: e.activation(out, in_, func, **kw), rd, wr)

    def tt(self, eng, out, in0, in1, op):
        return self.add(eng, lambda e: e.tensor_tensor(out, in0, in1, op), [in0, in1], [out])

    def ts(self, eng, out, in0, s1, op0, s2=None, op1=None):
        kw = {}
        if op1 is not None:
            kw["op1"] = op1
        return self.add(eng, lambda e: e.tensor_scalar(out, in0, s1, s2, op0, **kw), [in0, s1, s2], [out])

    def stt(self, eng, out, in0, scalar, in1, op0, op1):
        eng = "dve"
        return self.add(eng, lambda e: e.scalar_tensor_tensor(out, in0, scalar, in1, op0, op1), [in0, scalar, in1], [out])

    def copy(self, eng, out, in_):
        if eng == "act":
            return self.add("act", lambda e: e.copy(out, in_), [in_], [out])
        return self.add(eng, lambda e: e.tensor_copy(out, in_), [in_], [out])

    def recip(self, out, in_):
        return self.add("dve", lambda e: e.reciprocal(out, in_), [in_], [out])

    def memset(self, eng, out, val):
        return self.add(eng, lambda e: e.memset(out, val), [], [out])

    def dma(self, q, out, in_, **kw):
        return self.add(q, lambda e: e.dma_start(out=out, in_=in_, **kw), [in_], [out], dma=True)

    def finish(self):
        nc = self.nc
        fin = self.add("sp", lambda e: None, [], [])
        fin.pseudo = True
        last_sig = {}
        for i in self.ins[:-1]:
            if i.dma:
                last_sig[i.sig] = i.idx
        for j in last_sig.values():
            fin.deps.add(j)
        last_eng = {}
        for i in self.ins[:-1]:
            if not i.dma and not i.pseudo:
                last_eng[i.eng] = i.idx
        for e, j in last_eng.items():
            fin.deps.add(j)
            self.ins[j].signal = True
        sem_names = [("c", e) for e in self.ENGS] + [("d", e, k) for e in ("sp", "act", "pool") for k in range(N_DMA_SEMS)]
        stack = contextlib.ExitStack()
        sems = {}
        for sn in sem_names:
            sems[sn] = stack.enter_context(nc.semaphore("s_" + "_".join(str(x) for x in sn)))
        cnt = {sn: 0 for sn in sem_names}
        for i in self.ins:
            if i.dma:
                sn = ("d", i.eng, i.sig[1])
                cnt[sn] += 16
                i.sig = (sn, cnt[sn])
                i.inc = 16
            elif i.signal:
                sn = ("c", i.eng)
                cnt[sn] += 1
                i.sig = (sn, cnt[sn])
        know = {e: {} for e in self.ENGS}
        vc = {}
        for i in self.ins:
            kn = know[i.eng]
            need = {}
            for j in i.deps:
                sn, v = self.ins[j].sig
                if kn.get(sn, 0) < v and need.get(sn, 0) < v:
                    need[sn] = v
            for j in i.deps:
                for sn, v in vc[j].items():
                    if kn.get(sn, 0) < v:
                        kn[sn] = v
            i.waits = list(need.items())
            if i.signal:
                d = dict(kn)
                d[i.sig[0]] = i.sig[1]
                vc[i.idx] = d
        self.max_sem = max(cnt.values())
        per = {e: [i for i in self.ins if i.eng == e] for e in self.ENGS}

        def run(e, lst):
            for i in lst:
                for sn, v in i.waits:
                    e.wait_ge(sems[sn], v)
                r = i.emit(e)
                if i.signal and r is not None:
                    r.then_inc(sems[i.sig[0]], i.inc)

        with stack, nc.Block() as block:
            @block.tensor
            def _(e):
                run(e, per["pe"])

            @block.scalar
            def _(e):
                run(e, per["act"])

            @block.vector
            def _(e):
                run(e, per["dve"])

            @block.gpsimd
            def _(e):
                run(e, per["pool"])

            @block.sync
            def _(e):
                run(e, per["sp"])


class Ctx:
    pass


_UID = [0]


def pools(nc, st):
    _UID[0] += 1
    u = "_u%d" % _UID[0]

    def sb(n, s, d=F32):
        return st.enter_context(nc.sbuf_tensor(n + u, list(s), d))

    def ps(n, s, d=F32):
        return st.enter_context(nc.psum_tensor(n + u, list(s), d))
    return sb, ps


def load_weight_bf16(P, C, wb, w_dram, ncols, ln_pk, stage, nk=8, q="sp"):
    for k in range(nk):
        stg = stage[k % len(stage)]
        P.dma(q if k % 2 == 0 else "pool", stg[:, 0:ncols], w_dram[k * 128:(k + 1) * 128, :])
        eng = "dve" if k % 2 == 0 else "pool"
        if ln_pk is None:
            P.copy(eng, wb[:, k, 0:ncols], stg[:, 0:ncols])
        else:
            P.ts(eng, wb[:, k, 0:ncols], stg[:, 0:ncols], ln_pk[:, k:k + 1], ALU.mult)


def rms_to_hT(P, C, xt, n_tiles, hT, hb, pT, ss, rstd, junk):
    for n in range(n_tiles):
        P.act(junk[:], xt[:, n, :], AF.Square, accum_out=ss[:, n:n + 1])
        P.act(rstd[:, n:n + 1], ss[:, n:n + 1], AF.Sqrt, bias=C.epsb[:], scale=1.0 / D)
        P.recip(rstd[:, n:n + 1], rstd[:, n:n + 1])
        P.act(hb[:, n, :], xt[:, n, :], AF.Copy, scale=rstd[:, n:n + 1])
        for k in range(8):
            P.transpose(pT[:, k * 128:(k + 1) * 128], hb[:, n, k * 128:(k + 1) * 128], C.identb[:])
        P.copy("dve", hT[:, :, n * 128:(n + 1) * 128], pT[:].rearrange("p (k t) -> p k t", k=8))


def phase_inproj(nc, P, C, x_d, w_d, ncols, ln_pk_d, fm_jobs, tm_jobs):
    with contextlib.ExitStack() as st:
        sb, ps = pools(nc, st)
        wb = sb("ip_wb", [128, 8, ncols], BF16)
        stage = [sb("ip_stg%d" % i, [128, ncols]) for i in range(2)]
        lnpk = sb("ip_ln", [128, 8])
        xt = [sb("ip_xt%d" % i, [128, 4, D]) for i in range(2)]
        hb = sb("ip_hb", [128, 4, D], BF16)
        hT = [sb("ip_hT%d" % i, [128, 8, 512], BF16) for i in range(2)]
        junk = sb("ip_junk", [128, D])
        ss = sb("ip_ss", [128, 4])
        rstd = sb("ip_rstd", [128, 4])
        fst = {}
        pT = ps("ip_pT", [128, 1024], BF16)
        pm = [ps("ip_pm%d" % i, [128, 512]) for i in range(4)]
        P.dma("sp", lnpk[:], ln_pk_d)
        load_weight_bf16(P, C, wb, w_d, ncols, lnpk, stage)
        cnt = 0
        for tb in range(S // 512):
            x_t = xt[tb % 2]
            h_t = hT[tb % 2]
            P.dma("sp", x_t[:], x_d[tb * 512:(tb + 1) * 512, :].rearrange("(n p) d -> p n d", p=128))
            rms_to_hT(P, C, x_t, 4, h_t, hb, pT, ss, rstd, junk)
            for (col0, scale, sdt, pieces) in fm_jobs:
                pp = pm[cnt % 4]
                for k in range(8):
                    P.mm(pp[:], wb[:, k, col0:col0 + 128], h_t[:, k, :], start=(k == 0), stop=(k == 7))
                key = ("f", sdt, cnt % 3)
                if key not in fst:
                    fst[key] = sb("ip_fst%d_%s" % (cnt % 3, "b" if sdt == BF16 else "f"), [128, 512], sdt)
                sg = fst[key]
                if cnt % 2 == 0:
                    P.act(sg[:], pp[:], AF.Copy, scale=float(scale))
                else:
                    P.ts("dve", sg[:], pp[:], float(scale), ALU.mult)
                for (r0, r1, fn) in pieces:
                    P.dma("sp" if cnt % 2 == 0 else "pool", fn(tb), sg[r0:r1, :])
                cnt += 1
            for n in range(4):
                for (col0, width, func, sdt, fn) in tm_jobs:
                    pp = pm[cnt % 4]
                    for k in range(8):
                        P.mm(pp[:, 0:width], h_t[:, k, n * 128:(n + 1) * 128], wb[:, k, col0:col0 + width], start=(k == 0), stop=(k == 7))
                    key = ("t", sdt, cnt % 3)
                    if key not in fst:
                        fst[key] = sb("ip_tst%d_%s" % (cnt % 3, "b" if sdt == BF16 else "f"), [128, 512], sdt)
                    sg = fst[key]
                    if func is None:
                        P.copy("dve", sg[:, 0:width], pp[:, 0:width])
                    else:
                        P.act(sg[:, 0:width], pp[:, 0:width], func)
                    P.dma("sp" if cnt % 2 == 0 else "pool", fn(tb * 4 + n), sg[:, 0:width])
                    cnt += 1


def phase_pool(nc, P, C, T):
    with contextlib.ExitStack() as st:
        sb, ps = pools(nc, st)
        u = sb("pl_u", [128, S])
        a = sb("pl_a", [128, S])
        b = sb("pl_b", [128, S])
        yb = sb("pl_y", [128, S], BF16)
        pwf = sb("pl_pwf", [128, 4, 128])
        pwb = sb("pl_pwb", [128, 4, 128], BF16)
        psc = sb("pl_psc", [128, 4])
        fix = sb("pl_fix", [128, 4, 16])
        og = [sb("pl_o%d" % i, [128, 512], BF16) for i in range(2)]
        pp = [ps("pl_p%d" % i, [128, 512]) for i in range(2)]
        P.dma("sp", pwf[:], T.a_pool_w.rearrange("g c d -> c g d"))
        P.copy("dve", pwb[:], pwf[:])
        P.dma("sp", psc[:], T.a_pool_scale_pk)
        P.dma("sp", fix[:], T.poolfix)
        for g in range(4):
            P.dma("sp" if g % 2 == 0 else "pool", u[:], T.zT[g * 128:(g + 1) * 128, :])
            cur = u
            bufs = [a, b]
            for s in range(g + 1):
                sh = 1 << s
                nxt = bufs[s % 2]
                eng = "dve" if s % 2 == 0 else "pool"
                P.tt(eng, nxt[:, sh:S], cur[:, sh:S], cur[:, 0:S - sh], ALU.add)
                P.copy(eng, nxt[:, 0:sh], cur[:, 0:sh])
                cur = nxt
            win = 1 << (g + 1)
            P.tt("dve", cur[:, 0:16], cur[:, 0:16], fix[:, g, :], ALU.mult)
            P.stt("dve", yb[:], cur[:], 1.0 / win, u[:], ALU.mult, ALU.subtract)
            for tb in range(8):
                p_ = pp[tb % 2]
                P.mm(p_[:], pwb[:, g, :], yb[:, tb * 512:(tb + 1) * 512])
                o_ = og[tb % 2]
                P.act(o_[:], p_[:], AF.Copy, scale=psc[:, g:g + 1])
                P.dma("sp" if tb % 2 == 0 else "pool", T.yT[g * 128:(g + 1) * 128, tb * 512:(tb + 1) * 512], o_[:])


def phase_deltanet(nc, P, C, T):
    with contextlib.ExitStack() as st:
        sb, ps = pools(nc, st)
        raw = sb("dn_raw", [128, S + 3])
        cv = sb("dn_cv", [128, S])
        qg = sb("dn_qg", [128, S])
        kT = sb("dn_kT", [128, S])
        ktm = sb("dn_ktm", [128, NT, 128])
        vtm = sb("dn_vtm", [128, NT, 128])
        gcb = sb("dn_gcb", [128, S])
        cw = sb("dn_cw", [128, 12, 4])
        ba = sb("dn_ba", [128, NT, 8])
        beta = sb("dn_beta", [128, NT, 4])
        gtk = sb("dn_g", [128, NT, 4])
        gc = sb("dn_gc", [128, NT, 4])
        gcT = sb("dn_gcT", [128, 128])
        negA = sb("dn_negA", [128, 4])
        dtb = sb("dn_dtb", [128, 4])
        onorm = sb("dn_onorm", [128, 128])
        glast = sb("dn_glast", [128, NT, 4])
        sc1 = sb("dn_sc1", [128, NT, 4])
        sc2 = sb("dn_sc2", [128, NT, 4])
        tmp512 = [sb("dn_t512_%d" % i, [128, 512]) for i in range(2)]
        tA = [sb("dn_tA%d" % i, [128, 128]) for i in range(2)]
        tE = [sb("dn_tE%d" % i, [128, 128]) for i in range(2)]
        Am = [sb("dn_A%d" % i, [128, 128]) for i in range(2)]
        U = [sb("dn_U%d" % i, [128, 128]) for i in range(2)]
        L = [sb("dn_L%d" % i, [128, 128]) for i in range(2)]
        R = [sb("dn_R%d" % i, [128, 128]) for i in range(2)]
        attnT = cv[:].rearrange("p (n i) -> p n i", i=128)
        wT = raw[:, 0:S].rearrange("p (n i) -> p n i", i=128)
        uu = sb("dn_u", [128, NT, 128])
        kbg = [sb("dn_kbg%d" % i, [128, 128]) for i in range(2)]
        vb = [sb("dn_vb%d" % i, [128, 128]) for i in range(2)]
        kdA = sb("dn_kdA", [128, NT, 128])
        kdB = sb("dn_kdB", [128, NT, 128])
        St = sb("dn_S", [128, 128])
        vnew = [sb("dn_vn%d" % i, [128, 128]) for i in range(2)]
        ot = [sb("dn_ot%d" % i, [128, 128]) for i in range(2)]
        og = [sb("dn_og%d" % i, [128, 128]) for i in range(2)]
        ogb = [sb("dn_ogb%d" % i, [128, 128], BF16) for i in range(2)]
        gate = [sb("dn_gate%d" % i, [128, 128]) for i in range(2)]
        oss = sb("dn_oss", [128, 2])
        yst = [sb("dn_yst%d" % i, [128, 128], BF16) for i in range(2)]
        pb = [ps("dn_p%d" % i, [128, 512]) for i in range(6)]
        pTb = ps("dn_pT", [128, 1024], BF16)

        P.memset("pool", kdA[:], 0.0)
        P.memset("pool", kdB[:], 0.0)
        for v_ in vnew:
            P.memset("pool", v_[:], 0.0)
        P.dma("sp", cw[:], T.a_conv_w_pk)
        P.dma("sp", ba[:], T.ba.rearrange("(n p) c -> p n c", p=128))
        P.dma("sp", negA[:], T.a_a_log.partition_broadcast(128))
        P.dma("sp", dtb[:], T.a_dt_bias.partition_broadcast(128))
        P.dma("sp", onorm[:], T.a_o_norm.partition_broadcast(128))
        P.act(negA[:], negA[:], AF.Exp)
        P.ts("dve", negA[:], negA[:], -1.0, ALU.mult)
        P.act(beta[:], ba[:, :, 0:4], AF.Sigmoid)
        for n in range(NT):
            P.tt("dve", gtk[:, n, :], ba[:, n, 4:8], dtb[:], ALU.add)
        P.act(gtk[:], gtk[:], AF.Exp)
        P.act(gtk[:], gtk[:], AF.Ln, bias=1.0)
        for n in range(NT):
            P.tt("dve", gtk[:, n, :], gtk[:, n, :], negA[:], ALU.mult)
        P.mm(pb[0][:, 0:128], C.ltri[:], gtk[:].rearrange("p n h -> p (n h)"))
        P.copy("dve", gc[:].rearrange("p n h -> p (n h)"), pb[0][:, 0:128])
        P.mm(pb[1][:, 0:128], C.lend[:], gc[:].rearrange("p n h -> p (n h)"))
        P.copy("dve", glast[:].rearrange("p n h -> p (n h)"), pb[1][:, 0:128])
        P.transpose(pb[2][:, 0:128], gc[:].rearrange("p n h -> p (n h)"), C.identf[:])
        P.copy("dve", gcT[:], pb[2][:, 0:128])
        P.act(sc1[:], gc[:], AF.Exp)
        P.tt("dve", sc1[:], sc1[:], beta[:], ALU.mult)
        P.tt("dve", sc2[:], glast[:], gc[:], ALU.subtract)
        P.act(sc2[:], sc2[:], AF.Exp)
        P.act(glast[:], glast[:], AF.Exp)

        import os
        DN_STOP = int(os.environ.get("DN_STOP", "99"))
        for h in range(4 if DN_STOP > 10 else int(os.environ.get("DN_HEADS", "1"))):
            if DN_STOP < 1:
                break
            for n in range(NT):
                r = n * 4 + h
                pp = pb[n % 4]
                P.mm(pp[:, 0:128], C.identf[:, r:r + 1].to_broadcast([128, 128]), gcT[:])
                P.copy("dve" if n % 2 == 0 else "act", gcb[:, n * 128:(n + 1) * 128], pp[:, 0:128])
            for ty in range(3 if DN_STOP >= 2 else 0):
                ch = ty * 4 + h
                P.memset("pool", raw[:, 0:3], 0.0)
                P.dma("sp", raw[:, 3:S + 3], T.zT[512 + ch * 128:512 + (ch + 1) * 128, :])
                P.act(cv[:], raw[:, 3:S + 3], AF.Copy, scale=cw[:, ch, 3:4])
                for kk in range(3):
                    P.stt("dve" if kk != 1 else "pool", cv[:], raw[:, kk:S + kk], cw[:, ch, kk:kk + 1], cv[:], ALU.mult, ALU.add)
                P.act(cv[:], cv[:], AF.Silu)
                if ty < 2:
                    dst = qg if ty == 0 else kT
                    P.act(raw[:, 0:S], cv[:], AF.Square)
                    for tb in range(8):
                        pp = pb[tb % 4]
                        P.mm(pp[:], C.onesf[:], raw[:, tb * 512:(tb + 1) * 512])
                        tq = tmp512[tb % 2]
                        P.act(tq[:], pp[:], AF.Sqrt, bias=C.epsb[:], scale=1.0)
                        P.recip(tq[:], tq[:])
                        if ty == 0:
                            P.stt("dve", dst[:, tb * 512:(tb + 1) * 512], cv[:, tb * 512:(tb + 1) * 512], float(128 ** -0.5), tq[:], ALU.mult, ALU.mult)
                        else:
                            P.tt("dve", dst[:, tb * 512:(tb + 1) * 512], cv[:, tb * 512:(tb + 1) * 512], tq[:], ALU.mult)
                    if ty == 1:
                        for n in range(NT):
                            pp = pb[n % 4]
                            P.transpose(pp[:, 0:128], kT[:, n * 128:(n + 1) * 128], C.identf[:])
                            P.copy("dve" if n % 2 == 0 else "act", ktm[:, n, :], pp[:, 0:128])
                else:
                    for n in range(NT):
                        pp = pb[n % 4]
                        P.transpose(pp[:, 0:128], cv[:, n * 128:(n + 1) * 128], C.identf[:])
                        P.copy("dve" if n % 2 == 0 else "act", vtm[:, n, :], pp[:, 0:128])
            for n in range(NT if DN_STOP >= 3 else 0):
                cs = slice(n * 128, (n + 1) * 128)
                i2 = n % 2
                G = gcb[:, cs]
                pkk = pb[0]
                P.mm(pkk[:, 0:128], kT[:, cs], kT[:, cs])
                P.stt("dve", tA[i2][:], G, gc[:, n, h:h + 1], C.pm_strict[:], ALU.subtract, ALU.add)
                P.act(tE[i2][:], tA[i2][:], AF.Exp, scale=-1.0)
                P.stt("dve", Am[i2][:], pkk[:, 0:128], beta[:, n, h:h + 1], tE[i2][:], ALU.mult, ALU.mult)
                pqk = pb[1]
                P.mm(pqk[:, 0:128], kT[:, cs], qg[:, cs])
                P.stt("pool", tA[i2][:], G, gc[:, n, h:h + 1], C.nm_upper[:], ALU.subtract, ALU.add)
                P.act(tE[i2][:], tA[i2][:], AF.Exp)
                P.tt("dve", attnT[:, n, :], pqk[:, 0:128], tE[i2][:], ALU.mult)
                ptr = pb[2]
                P.transpose(ptr[:, 0:128], Am[i2][:], C.identf[:])
                Uc, Lc, Rc = U[0], Am[i2], R[0]
                P.copy("act", Uc[:], ptr[:, 0:128])
                P.tt("dve", Rc[:], C.identf[:], ptr[:, 0:128], ALU.subtract)
                for lev in range(1, 6):
                    Un, Ln_, Rn = U[lev % 2], L[lev % 2], R[lev % 2]
                    pu, pl, pr = pb[3], pb[4], pb[5]
                    P.mm(pu[:, 0:128], Lc[:], Uc[:])
                    P.mm(pl[:, 0:128], Uc[:], Lc[:])
                    P.copy("act", Un[:], pu[:, 0:128])
                    P.copy("dve", Ln_[:], pl[:, 0:128])
                    P.mm(pr[:, 0:128], Ln_[:], Rc[:])
                    P.tt("dve", Rn[:], Rc[:], pr[:, 0:128], ALU.add)
                    Uc, Lc, Rc = Un, Ln_, Rn
                P.ts("pool", kbg[i2][:], ktm[:, n, :], sc1[:, n, h:h + 1], ALU.mult)
                P.ts("pool", vb[i2][:], vtm[:, n, :], beta[:, n, h:h + 1], ALU.mult)
                P.ts("pool", kdA[0:64, n, :], ktm[0:64, n, :], sc2[0:64, n, h:h + 1], ALU.mult)
                P.ts("pool", kdB[64:128, n, :], ktm[64:128, n, :], sc2[64:128, n, h:h + 1], ALU.mult)
                pw, pu2 = pb[0], pb[1]
                P.mm(pw[:, 128:256], kbg[i2][:], Rc[:])
                P.copy("act", wT[:, n, :], pw[:, 128:256])
                P.mm(pu2[:, 128:256], Rc[:], vb[i2][:])
                P.copy("dve", uu[:, n, :], pu2[:, 128:256])
            for tb in range(8):
                tq = tmp512[tb % 2]
                P.act(tq[:], gcb[:, tb * 512:(tb + 1) * 512], AF.Exp)
                P.tt("dve", qg[:, tb * 512:(tb + 1) * 512], qg[:, tb * 512:(tb + 1) * 512], tq[:], ALU.mult)
            P.memset("dve", St[:], 0.0)
            for n in range(NT if DN_STOP >= 4 else 0):
                i2 = n % 2
                P.dma("pool", gate[i2][:], T.gate[n * 128:(n + 1) * 128, h * 128:(h + 1) * 128])
                for b in range(2):
                    rs = slice(64 * b, 64 * b + 64)
                    cs = slice(n * 128 + 64 * b, n * 128 + 64 * b + 64)
                    p1, p2, p3 = pb[2], pb[3], pb[4]
                    P.mm(p1[rs, 0:128], wT[:, n, rs], St[:])
                    P.tt("dve", vnew[i2][rs, :], uu[rs, n, :], p1[rs, 0:128], ALU.subtract)
                    P.mm(p2[rs, 0:128], qg[:, cs], St[:], start=True, stop=False)
                    P.mm(p2[rs, 0:128], attnT[:, n, rs], vnew[i2][:], start=False, stop=True)
                    P.copy("act", ot[i2][rs, :], p2[rs, 0:128])
                    P.mm(p3[:, 0:128], (kdA if b == 0 else kdB)[:, n, :], vnew[i2][:])
                    P.act(C.scal[:, 0:1], gcb[:, n * 128 + 64 * b + 63:n * 128 + 64 * b + 64], AF.Exp)
                    P.stt("dve", St[:], St[:], C.scal[:, 0:1], p3[:, 0:128], ALU.mult, ALU.add)
                P.act(og[i2][:], ot[i2][:], AF.Square, accum_out=oss[:, 0:1])
                P.act(oss[:, 1:2], oss[:, 0:1], AF.Sqrt, bias=C.epsb[:], scale=1.0 / 128)
                P.recip(oss[:, 1:2], oss[:, 1:2])
                P.stt("dve", og[i2][:], ot[i2][:], oss[:, 1:2], onorm[:], ALU.mult, ALU.mult)
                P.tt("dve", ogb[i2][:], og[i2][:], gate[i2][:], ALU.mult)
                P.transpose(pTb[:, 0:128], ogb[i2][:], C.identb[:])
                P.copy("act", yst[i2][:], pTb[:, 0:128])
                P.dma("sp", T.yT[512 + h * 128:512 + (h + 1) * 128, n * 128:(n + 1) * 128], yst[i2][:])


def phase_xattn(nc, P, C, T, x_in, x_out, w_out_d, l):
    with contextlib.ExitStack() as st:
        sb, ps = pools(nc, st)
        wo_b = sb("xa_wout", [128, 8, D], BF16)
        wq_b = sb("xa_wq", [128, 8, D], BF16)
        wo2_b = sb("xa_wo", [128, 8, D], BF16)
        stage = [sb("xa_stg%d" % i, [128, D]) for i in range(2)]
        lnq = sb("xa_lnq", [128, 8])
        lnm = sb("xa_lnm", [128, 8])
        kTm = sb("xa_kT", [128, 8, 256], BF16)
        vm = sb("xa_v", [128, 2, D], BF16)
        mt = sb("xa_mt", [128, 2, D])
        hbm = sb("xa_hbm", [128, 4, D], BF16)
        hTm = sb("xa_hTm", [128, 8, 256], BF16)
        xt = [sb("xa_xt0", [128, 4, D])] * 2
        x1 = sb("xa_x1", [128, 4, D])
        yTt = [sb("xa_yT%d" % i, [128, 8, 512], BF16) for i in range(2)]
        hT = sb("xa_hT", [128, 8, 512], BF16)
        qT = sb("xa_qT", [128, 8, 512], BF16)
        pTt = sb("xa_pT", [128, 2, 512], BF16)
        rinv = sb("xa_rinv", [128, 512])
        oT = sb("xa_oT", [128, 8, 512], BF16)
        junk = sb("xa_junk", [128, D])
        ss = sb("xa_ss", [128, 4])
        rstd = sb("xa_rstd", [128, 4])
        pT = ps("xa_pTp", [128, 1024], BF16)
        pm = [ps("xa_pm%d" % i, [128, 512]) for i in range(6)]
        P.dma("sp", lnq[:], T.xa_ln_pk[l])
        P.dma("sp", lnm[:], T.xa_mem_ln_pk[l])
        load_weight_bf16(P, C, wq_b, T.xa_wk[l], D, lnm, stage)
        load_weight_bf16(P, C, wo2_b, T.xa_wv[l], D, lnm, stage)
        P.dma("sp", mt[:], T.mem.rearrange("(n p) d -> p n d", p=128))
        rms_to_hT(P, C, mt, 2, hTm, hbm, pT, ss, rstd, junk)
        for c in range(8):
            pp = pm[c % 4]
            for k in range(8):
                P.mm(pp[:, 0:256], wq_b[:, k, c * 128:(c + 1) * 128], hTm[:, k, :], start=(k == 0), stop=(k == 7))
            P.copy("act" if c % 2 == 0 else "dve", kTm[:, c, :], pp[:, 0:256])
        for mtile in range(2):
            for half in range(2):
                pp = pm[(mtile * 2 + half) % 4]
                for k in range(8):
                    P.mm(pp[:], hTm[:, k, mtile * 128:(mtile + 1) * 128], wo2_b[:, k, half * 512:(half + 1) * 512], start=(k == 0), stop=(k == 7))
                P.copy("act" if half == 0 else "dve", vm[:, mtile, half * 512:(half + 1) * 512], pp[:])
        load_weight_bf16(P, C, wo_b, w_out_d, D, None, stage)
        load_weight_bf16(P, C, wq_b, T.xa_wq[l], D, lnq, stage)
        load_weight_bf16(P, C, wo2_b, T.xa_wo[l], D, None, stage)
        cnt = 0
        for tb in range(S // 512):
            x_t = xt[tb % 2]
            y_t = yTt[tb % 2]
            P.dma("sp", x_t[:], x_in[tb * 512:(tb + 1) * 512, :].rearrange("(n p) d -> p n d", p=128))
            P.dma("pool", y_t[:], T.yT[:, tb * 512:(tb + 1) * 512].rearrange("(k p) t -> p k t", p=128))
            for n in range(4):
                for half in range(2):
                    pp = pm[cnt % 6]
                    cnt += 1
                    for k in range(8):
                        P.mm(pp[:], y_t[:, k, n * 128:(n + 1) * 128], wo_b[:, k, half * 512:(half + 1) * 512], start=(k == 0), stop=(k == 7))
                    P.tt("dve", x1[:, n, half * 512:(half + 1) * 512], x_t[:, n, half * 512:(half + 1) * 512], pp[:], ALU.add)
            import os
            if os.environ.get("XA_STOP") == "1":
                P.dma("sp", x_out[tb * 512:(tb + 1) * 512, :].rearrange("(n p) d -> p n d", p=128), x1[:])
                continue
            rms_to_hT(P, C, x1, 4, hT, hbm, pT, ss, rstd, junk)
            for c in range(8):
                pp = pm[cnt % 6]
                cnt += 1
                for k in range(8):
                    P.mm(pp[:], wq_b[:, k, c * 128:(c + 1) * 128], hT[:, k, :], start=(k == 0), stop=(k == 7))
                P.copy("act" if c % 2 == 0 else "dve", qT[:, c, :], pp[:])
            for hd in range(4):
                pd = pm[cnt % 6]
                cnt += 1
                for mtile in range(2):
                    pp = pm[cnt % 6]
                    cnt += 1
                    for dc in range(2):
                        P.mm(pp[:], kTm[:, hd * 2 + dc, mtile * 128:(mtile + 1) * 128], qT[:, hd * 2 + dc, :], start=(dc == 0), stop=(dc == 1))
                    P.act(pTt[:, mtile, :], pp[:], AF.Exp, scale=1.0 / 16.0)
                for mtile in range(2):
                    P.mm(pd[:], C.onesb[:], pTt[:, mtile, :], start=(mtile == 0), stop=(mtile == 1))
                P.recip(rinv[:], pd[:])
                for dc in range(2):
                    pp = pm[cnt % 6]
                    cnt += 1
                    for mtile in range(2):
                        P.mm(pp[:], vm[:, mtile, (hd * 2 + dc) * 128:(hd * 2 + dc + 1) * 128], pTt[:, mtile, :], start=(mtile == 0), stop=(mtile == 1))
                    P.tt("dve", oT[:, hd * 2 + dc, :], pp[:], rinv[:], ALU.mult)
            for n in range(4):
                for half in range(2):
                    pp = pm[cnt % 6]
                    cnt += 1
                    for k in range(8):
                        P.mm(pp[:], oT[:, k, n * 128:(n + 1) * 128], wo2_b[:, k, half * 512:(half + 1) * 512], start=(k == 0), stop=(k == 7))
                    P.tt("dve", x1[:, n, half * 512:(half + 1) * 512], x1[:, n, half * 512:(half + 1) * 512], pp[:], ALU.add)
            P.dma("sp", x_out[tb * 512:(tb + 1) * 512, :].rearrange("(n p) d -> p n d", p=128), x1[:])


def phase_ffn(nc, P, C, T, x_in, x_out, l, final):
    TB = 256
    with contextlib.ExitStack() as st:
        sb, ps = pools(nc, st)
        w1 = sb("ff_w1", [128, 8, 4096], BF16)
        w2 = sb("ff_w2", [128, 32, D], BF16)
        stage = [sb("ff_stg%d" % i, [128, 1024]) for i in range(2)]
        ln = sb("ff_ln", [128, 8])
        fl = sb("ff_fl", [128, D])
        xt = [sb("ff_xt0", [128, 2, D])] * 2
        hb = sb("ff_hb", [128, 2, D], BF16)
        hT = sb("ff_hT", [128, 8, TB], BF16)
        h1 = sb("ff_h1", [128, 32, TB], BF16)
        sq = [sb("ff_sq%d" % i, [128, TB]) for i in range(2)]
        xo = [sb("ff_xo0", [128, 2, D])] * 2
        junk = sb("ff_junk", [128, D])
        ss = sb("ff_ss", [128, 4])
        rstd = sb("ff_rstd", [128, 4])
        pT = ps("ff_pT", [128, 1024], BF16)
        pm = [ps("ff_pm%d" % i, [128, 512]) for i in range(6)]
        P.dma("sp", ln[:], T.ff_ln_pk[l])
        if final:
            P.dma("sp", fl[:], T.final_ln.partition_broadcast(128))
        for k in range(8):
            for qd in range(4):
                stg = stage[(k * 4 + qd) % 2]
                P.dma("sp" if qd % 2 == 0 else "pool", stg[:], T.ff_w1[l][k * 128:(k + 1) * 128, qd * 1024:(qd + 1) * 1024])
                P.ts("dve" if qd % 2 == 0 else "pool", w1[:, k, qd * 1024:(qd + 1) * 1024], stg[:], ln[:, k:k + 1], ALU.mult)
        for f in range(32):
            stg = stage[f % 2]
            P.dma("sp" if f % 2 == 0 else "pool", stg[:], T.ff_w2[l][f * 128:(f + 1) * 128, :])
            P.copy("dve" if f % 2 == 0 else "pool", w2[:, f, :], stg[:])
        cnt = 0
        for tb in range(S // TB):
            x_t = xt[tb % 2]
            x_o = xo[tb % 2]
            P.dma("sp", x_t[:], x_in[tb * TB:(tb + 1) * TB, :].rearrange("(n p) d -> p n d", p=128))
            rms_to_hT(P, C, x_t, 2, hT, hb, pT, ss, rstd, junk)
            for f in range(32):
                pp = pm[cnt % 6]
                cnt += 1
                for k in range(8):
                    P.mm(pp[:, 0:TB], w1[:, k, f * 128:(f + 1) * 128], hT[:, k, :], start=(k == 0), stop=(k == 7))
                s_ = sq[f % 2]
                P.act(s_[:], pp[:, 0:TB], AF.Square)
                P.stt("dve", h1[:, f, :], pp[:, 0:TB], 0.0, s_[:], ALU.is_gt, ALU.mult)
            for n in range(2):
                for half in range(2):
                    pp = pm[cnt % 6]
                    cnt += 1
                    for f in range(32):
                        P.mm(pp[:], h1[:, f, n * 128:(n + 1) * 128], w2[:, f, half * 512:(half + 1) * 512], start=(f == 0), stop=(f == 31))
                    P.tt("dve", x_o[:, n, half * 512:(half + 1) * 512], x_t[:, n, half * 512:(half + 1) * 512], pp[:], ALU.add)
            if final:
                for n in range(2):
                    P.act(junk[:], x_o[:, n, :], AF.Square, accum_out=ss[:, n:n + 1])
                    P.act(rstd[:, n:n + 1], ss[:, n:n + 1], AF.Sqrt, bias=C.epsb[:], scale=1.0 / D)
                    P.recip(rstd[:, n:n + 1], rstd[:, n:n + 1])
                    P.stt("dve", x_o[:, n, :], x_o[:, n, :], rstd[:, n:n + 1], fl[:], ALU.mult, ALU.mult)
            P.dma("pool", x_out[tb * TB:(tb + 1) * TB, :].rearrange("(n p) d -> p n d", p=128), x_o[:])


def make_consts():
    i = np.arange(128)
    same = (i[:, None] // 64) == (i[None, :] // 64)
    cf = {}
    cf["identf"] = np.eye(128, dtype=np.float32)
    cf["onesf"] = np.ones((128, 128), np.float32)
    cf["ltri"] = (same & (i[:, None] <= i[None, :])).astype(np.float32)
    cf["lend"] = (i[:, None] == (i[None, :] // 64) * 64 + 63).astype(np.float32)
    cf["pm_strict"] = np.where(same & (i[None, :] < i[:, None]), 0.0, BIG).astype(np.float32)
    cf["nm_upper"] = np.where(same & (i[None, :] >= i[:, None]), 0.0, -BIG).astype(np.float32)
    return cf


CF_ORDER = ["identf", "onesf", "ltri", "lend", "pm_strict", "nm_upper"]


def pk(v):
    return np.ascontiguousarray(np.asarray(v, np.float32).reshape(-1, 128).T)


def host_inputs(inp):
    sh = {}
    cf = make_consts()
    sh["cf"] = np.concatenate([cf[k] for k in CF_ORDER], axis=1)
    pf = np.ones((4, 16), np.float32)
    for g in range(4):
        win = 2 << g
        for t in range(win - 1):
            pf[g, t] = win / (t + 1.0)
    sh["poolfix"] = np.ascontiguousarray(np.broadcast_to(pf[None], (128, 4, 16)))
    f = lambda k: np.ascontiguousarray(np.asarray(inp[k], np.float32))
    sh["a_ln_pk"] = pk(f("a_ln")[0])
    sh["a_w_in"] = f("a_w_in")[0]
    sh["a_pool_w"] = f("a_pool_w")[0]
    sh["a_pool_scale_pk"] = pk(f("a_pool_scale")[0])
    sh["a_conv_w_pk"] = np.ascontiguousarray(f("a_conv_w")[0].reshape(4, 12, 128).transpose(2, 1, 0))
    sh["a_a_log"] = f("a_a_log")
    sh["a_dt_bias"] = f("a_dt_bias")
    sh["a_o_norm"] = f("a_o_norm")
    sh["a_w_out"] = f("a_w_out")[0]
    sh["c_ln_pk"] = pk(f("c_ln")[0])
    sh["c_w_in"] = f("c_w_in")[0]
    sh["c_w_out"] = f("c_w_out")[0]
    for l in range(2):
        sh["xa_ln_pk%d" % l] = pk(f("xa_ln")[l])
        sh["xa_mem_ln_pk%d" % l] = pk(f("xa_mem_ln")[l])
        sh["ff_ln_pk%d" % l] = pk(f("ff_ln")[l])
        for w in ("xa_wq", "xa_wk", "xa_wv", "xa_wo", "ff_w1", "ff_w2"):
            sh["%s%d" % (w, l)] = f(w)[l]
    sh["final_ln"] = f("final_ln").reshape(1, D)
    for kv in ("k", "v"):
        sh["c_w1_" + kv] = f("c_w1_" + kv)[0]
        sh["c_w2_" + kv] = f("c_w2_" + kv)[0]
        sh["c_pe_%sT" % kv] = np.ascontiguousarray(f("c_pe_" + kv)[0].T)
    sh.update(nsa_consts())
    return sh


def build(upto=99, dbg=()):
    nc = bass.Bass("TRN2", target_bir_lowering=False)
    T = Ctx()
    C = Ctx()

    def inp(name, shape, dt=F32):
        return nc.dram_tensor(name, list(shape), dt, kind="ExternalInput").ap()

    def scr(name, shape, dt=F32):
        return nc.dram_tensor(name, list(shape), dt, kind="ExternalOutput" if name in dbg else "Internal").ap()

    x_d = inp("x", [S, D])
    T.mem = inp("mem", [256, D])
    cf_d = inp("cf", [128, 128 * len(CF_ORDER)])
    T.poolfix = inp("poolfix", [128, 4, 16])
    a_ln_pk = inp("a_ln_pk", [128, 8])
    a_w_in = inp("a_w_in", [D, 2568])
    T.a_pool_w = inp("a_pool_w", [4, 128, 128])
    T.a_pool_scale_pk = inp("a_pool_scale_pk", [128, 4])
    T.a_conv_w_pk = inp("a_conv_w_pk", [128, 12, 4])
    T.a_a_log = inp("a_a_log", [1, 4])
    T.a_dt_bias = inp("a_dt_bias", [1, 4])
    T.a_o_norm = inp("a_o_norm", [1, 128])
    a_w_out = inp("a_w_out", [D, D])
    c_ln_pk = inp("c_ln_pk", [128, 8])
    c_w_in = inp("c_w_in", [D, 2608])
    c_w_out = inp("c_w_out", [D, D])
    T.xa_ln_pk = [inp("xa_ln_pk%d" % l, [128, 8]) for l in range(2)]
    T.xa_mem_ln_pk = [inp("xa_mem_ln_pk%d" % l, [128, 8]) for l in range(2)]
    T.ff_ln_pk = [inp("ff_ln_pk%d" % l, [128, 8]) for l in range(2)]
    T.xa_wq = [inp("xa_wq%d" % l, [D, D]) for l in range(2)]
    T.xa_wk = [inp("xa_wk%d" % l, [D, D]) for l in range(2)]
    T.xa_wv = [inp("xa_wv%d" % l, [D, D]) for l in range(2)]
    T.xa_wo = [inp("xa_wo%d" % l, [D, D]) for l in range(2)]
    T.ff_w1 = [inp("ff_w1%d" % l, [D, 4096]) for l in range(2)]
    T.ff_w2 = [inp("ff_w2%d" % l, [4096, D]) for l in range(2)]
    T.final_ln = inp("final_ln", [1, D])
    T.c_w1_k = inp("c_w1_k", [2048, 128]); T.c_w2_k = inp("c_w2_k", [128, 64]); T.c_pe_kT = inp("c_pe_kT", [64, 32])
    T.c_w1_v = inp("c_w1_v", [2048, 128]); T.c_w2_v = inp("c_w2_v", [128, 64]); T.c_pe_vT = inp("c_pe_vT", [64, 32])
    T.qaug = inp("qaug", [7, 16, S], BF16); T.kaug = inp("kaug", [7, S], BF16); T.kaug_cmp = inp("kaug_cmp", [7, 256], BF16)
    T.Etab = inp("Etab", [64, NT, 128], BF16); T.cmpmask = inp("cmpmask", [128, 6144], BF16)
    T.cz = inp("cz", [128, 128], BF16); T.wz = inp("wz", [128, 128], BF16)
    T.mulm = inp("mulm", [128, NT, 64]); T.addm = inp("addm", [128, NT, 64]); T.ovl = inp("ovl", [128, 2, 64], BF16)
    out_d = nc.dram_tensor("out", [S, D], F32, kind="ExternalOutput").ap()

    T.zT = scr("zT", [2048, S])
    T.gate = scr("gate", [S, 512])
    T.ba = scr("ba", [S, 8])
    T.yT = scr("yT", [D, S], BF16)
    xa = scr("xa", [S, D])
    xb = scr("xb", [S, D])
    T.qT = scr("qT", [16, 64, S], BF16)
    T.kcT = scr("kcT", [4, 64, S], BF16)
    T.vcT = scr("vcT", [4, 64, S], BF16)
    T.ksT = scr("ksT", [4, 64, S], BF16)
    T.kwT = scr("kwT", [4, 64, S], BF16)
    T.vs = scr("vs", [S, 256], BF16)
    T.vw = scr("vw", [S, 256], BF16)
    T.gates = scr("gates", [S, 48])

    P = Prog(nc)
    with contextlib.ExitStack() as st:
        sb, ps = pools(nc, st)
        cft = sb("c_cf", [128, 128 * len(CF_ORDER)])
        P.dma("sp", cft[:], cf_d)
        for i, k in enumerate(CF_ORDER):
            setattr(C, k, cft[:, i * 128:(i + 1) * 128])
        identb = sb("c_identb", [128, 128], BF16)
        onesb = sb("c_onesb", [128, 128], BF16)
        epsb = sb("c_epsb", [128, 1])
        scal = sb("c_scal", [128, 4])
        C.identb, C.onesb, C.epsb, C.scal = identb, onesb, epsb, scal
        P.copy("dve", identb[:], C.identf)
        P.copy("dve", onesb[:], C.onesf)
        P.memset("dve", epsb[:], 1e-6)

        phases = []
        fm0 = [(c * 128, 1.0, F32, [(0, 128, (lambda tb, c=c: T.zT[c * 128:(c + 1) * 128, tb * 512:(tb + 1) * 512]))]) for c in range(16)]
        tm0 = [(2048, 512, AF.Silu, F32, lambda t: T.gate[t * 128:(t + 1) * 128, :]),
               (2560, 8, None, F32, lambda t: T.ba[t * 128:(t + 1) * 128, :])]
        phases.append(lambda: phase_inproj(nc, P, C, x_d, a_w_in, 2568, a_ln_pk, fm0, tm0))
        phases.append(lambda: phase_pool(nc, P, C, T))
        phases.append(lambda: phase_deltanet(nc, P, C, T))
        phases.append(lambda: phase_xattn(nc, P, C, T, x_d, xa, a_w_out, 0))
        phases.append(lambda: phase_ffn(nc, P, C, T, xa, xb, 0, False))
        def two(dst, j):
            return [(0, 64, (lambda tb, d=dst, j=j: d[2 * j, :, tb * 512:(tb + 1) * 512])),
                    (64, 128, (lambda tb, d=dst, j=j: d[2 * j + 1, :, tb * 512:(tb + 1) * 512]))]
        fm1 = [(c * 128, 0.125, BF16, two(T.qT, c)) for c in range(8)]
        for base, dst in ((1024, T.kcT), (1280, T.vcT), (1536, T.ksT), (2048, T.kwT)):
            for j in range(2):
                fm1.append((base + j * 128, 1.0, BF16, two(dst, j)))
        tm1 = [(1792, 256, None, BF16, lambda t: T.vs[t * 128:(t + 1) * 128, :]),
               (2304, 256, None, BF16, lambda t: T.vw[t * 128:(t + 1) * 128, :]),
               (2560, 48, AF.Sigmoid, F32, lambda t: T.gates[t * 128:(t + 1) * 128, :])]
        phases.append(lambda: phase_inproj(nc, P, C, xb, c_w_in, 2608, c_ln_pk, fm1, tm1))
        phases.append(lambda: phase_nsa(nc, P, C, T))
        phases.append(lambda: phase_xattn(nc, P, C, T, xb, xa, c_w_out, 1))
        phases.append(lambda: phase_ffn(nc, P, C, T, xa, out_d, 1, True))
        for i, ph in enumerate(phases):
            if i < upto:
                ph()
                P.barrier()
        P.finish()
    return nc, P


def kernel(**inputs):
    sh = host_inputs(inputs)
    x = np.asarray(inputs["x"], np.float32)
    mem = np.asarray(inputs["mem"], np.float32)
    nc, P = build()
    in_maps = []
    for b in range(8):
        m = dict(sh)
        m["x"] = np.ascontiguousarray(x[b])
        m["mem"] = np.ascontiguousarray(mem[b])
        in_maps.append(m)
    res = run_bass_kernel_spmd(nc, in_maps, core_ids=list(range(8)))
    return np.stack([np.asarray(r["out"], np.float32) for r in res.results], axis=0)


def bcast4(ap128):
    return bass.AP(ap128.tensor, ap128.offset, [list(ap128.ap[0]), [0, 4], [1, 128]])


def phase_nsa(nc, P, C, T):
    with contextlib.ExitStack() as st:
        sb, ps = pools(nc, st)
        ksTa = sb("ns_ksTa", [71, 4, S], BF16)
        kwTa = sb("ns_kwTa", [71, 4, S], BF16)
        vsa = sb("ns_vsa", [128, NT, 4, 65], BF16)
        vwa = sb("ns_vwa", [128, NT, 4, 65], BF16)
        ckTa = sb("ns_ckTa", [71, 4, 256], BF16)
        cva = sb("ns_cva", [128, 2, 4, 129], BF16)
        Etab = sb("ns_E", [64, NT, 128], BF16)
        cmpm = sb("ns_cmpm", [128, 6144], BF16)
        cz = sb("ns_cz", [128, 128], BF16)
        wz = sb("ns_wz", [128, 128], BF16)
        mulm = sb("ns_mulm", [128, NT, 64])
        addm = sb("ns_addm", [128, NT, 64])
        gsb = sb("ns_gates", [128, NT, 48])
        ovl = sb("ns_ovl", [128, 2, 64], BF16)
        Qa = [sb("ns_Qa%d" % i, [71, 16, 128], BF16) for i in range(2)]
        pt = [sb("ns_pt%d" % i, [128, 512], BF16) for i in range(3)]
        oacc = sb("ns_oacc", [128, 16, 64])
        ob = sb("ns_ob", [128, D], BF16)
        oTs = sb("ns_oTs", [128, 8, 128], BF16)
        dn = sb("ns_dn", [128, 4])
        fr = sb("ns_fr", [128, 4])
        imp = sb("ns_imp", [128, 64])
        impm = sb("ns_impm", [128, 64])
        impw = sb("ns_impw", [128, 64])
        m8 = sb("ns_m8", [128, 8])
        m8b = sb("ns_m8b", [128, 8])
        sel = sb("ns_sel", [128, 64])
        mneg = sb("ns_mneg", [128, 64], BF16)
        MT = sb("ns_MT", [64, 4, 128], BF16)
        w1b = sb("ns_w1b", [64, 32, 128], BF16)
        w1f = sb("ns_w1f", [64, 32, 128])
        w2f = sb("ns_w2f", [128, 64])
        w2b = sb("ns_w2b", [128, 64], BF16)
        peT = sb("ns_peT", [64, 32])
        peTb = sb("ns_peTb", [64, 32], BF16)
        cb = sb("ns_cb", [128, 1])
        kct = sb("ns_kct", [64, S], BF16)
        hid = sb("ns_hid", [128, 256], BF16)
        pss = [ps("ns_ps%d" % i, [128, 512]) for i in range(2)]
        acc = [ps("ns_acc%d" % i, [128, 512]) for i in range(4)]
        ptr = ps("ns_ptr", [128, 1024], BF16)

        for g in range(4):
            P.dma("sp", ksTa[0:64, g, :], T.ksT[g])
            P.dma("pool", kwTa[0:64, g, :], T.kwT[g])
            P.dma("sp", ksTa[64:71, g, :], T.kaug)
            P.dma("pool", kwTa[64:71, g, :], T.kaug)
            P.dma("sp", ckTa[64:71, g, :], T.kaug_cmp)
        P.memset("pool", vsa[:, :, :, 64:65], 1.0)
        P.memset("pool", vwa[:, :, :, 64:65], 1.0)
        for n in range(NT):
            P.dma("sp" if n % 2 == 0 else "pool", vsa[:, n, :, 0:64], T.vs[n * 128:(n + 1) * 128, :].rearrange("p (g d) -> p g d", g=4))
            P.dma("pool" if n % 2 == 0 else "sp", vwa[:, n, :, 0:64], T.vw[n * 128:(n + 1) * 128, :].rearrange("p (g d) -> p g d", g=4))
        P.dma("sp", Etab[:], T.Etab)
        P.dma("pool", cmpm[:], T.cmpmask)
        P.dma("sp", cz[:], T.cz)
        P.dma("sp", wz[:], T.wz)
        P.dma("sp", mulm[:], T.mulm)
        P.dma("pool", addm[:], T.addm)
        P.dma("sp", gsb[:], T.gates.rearrange("(n p) c -> p n c", p=128))
        P.dma("sp", ovl[:], T.ovl)
        P.memset("dve", hid[:], 0.0)
        P.memset("dve", cva[:, :, :, 64:65], 1.0)
        for g in range(4):
            for ct in range(2):
                P.copy("pool", cva[:, ct, g, 65:129], ovl[:, ct, :])
        for kv in range(2):
            w1_d, w2_d, pe_d, src = (T.c_w1_k, T.c_w2_k, T.c_pe_kT, T.kcT) if kv == 0 else (T.c_w1_v, T.c_w2_v, T.c_pe_vT, T.vcT)
            P.dma("sp", w1f[:], w1_d.rearrange("(l d) h -> d l h", d=64))
            P.copy("dve", w1b[:], w1f[:])
            P.dma("sp", w2f[:], w2_d)
            P.copy("dve", w2b[:], w2f[:])
            P.dma("sp", peT[:], pe_d)
            P.copy("dve", peTb[:], peT[:])
            pc = acc[0]
            for l in range(32):
                P.mm(pc[:, 0:1], w1b[:, l, :], peTb[:, l:l + 1], start=(l == 0), stop=(l == 31))
            P.copy("dve", cb[:], pc[:, 0:1])
            for g in range(4):
                P.dma("sp", kct[:], src[g])
                ph = acc[1 + g % 2]
                for l in range(32):
                    rhs = bass.AP(kct[:].tensor, kct[:, l:l + 1].offset, [list(kct[:].ap[0]), [16, 255]])
                    P.mm(ph[:, 0:255], w1b[:, l, :], rhs, start=(l == 0), stop=(l == 31))
                P.act(hid[:, 0:255], ph[:, 0:255], AF.Silu, bias=cb[:])
                if kv == 0:
                    po = acc[3]
                    P.mm(po[0:64, 0:256], w2b[:], hid[:])
                    P.copy("dve", ckTa[0:64, g, :], po[0:64, 0:256])
                else:
                    for ct in range(2):
                        po = acc[3]
                        P.mm(po[:, 0:64], hid[:, ct * 128:(ct + 1) * 128], w2b[:])
                        P.copy("dve", cva[:, ct, g, 0:64], po[:, 0:64])

        unit = [0]

        def scores(lhsT, rhsQ, extra):
            p_ = pss[unit[0] % 2]
            t_ = pt[unit[0] % 3]
            unit[0] += 1
            P.mm(p_[:], lhsT, rhsQ, start=True, stop=(len(extra) == 0))
            for i, (l2, r2) in enumerate(extra):
                P.mm(p_[:], l2, r2, start=False, stop=(i == len(extra) - 1))
            P.act(t_[:], p_[:], AF.Exp)
            return t_

        def finalize(g, qt, br, ncols, first):
            for r in range(4):
                h = 4 * g + r
                P.ts("dve", dn[:, r:r + 1], acc[r][:, 64:65], 1e-30, ALU.max)
                P.recip(dn[:, r:r + 1], dn[:, r:r + 1])
                if br == 0:
                    if r == 0:
                        P.ts("dve", imp[:], acc[r][:, 65:129], dn[:, r:r + 1], ALU.mult)
                    else:
                        P.stt("dve", imp[:], acc[r][:, 65:129], dn[:, r:r + 1], imp[:], ALU.mult, ALU.add)
                gc_ = g * 12 + r * 3 + br
                P.tt("dve", fr[:, r:r + 1], dn[:, r:r + 1], gsb[:, qt, gc_:gc_ + 1], ALU.mult)
                if first:
                    P.ts("dve", oacc[:, h, :], acc[r][:, 0:64], fr[:, r:r + 1], ALU.mult)
                else:
                    P.stt("dve", oacc[:, h, :], acc[r][:, 0:64], fr[:, r:r + 1], oacc[:, h, :], ALU.mult, ALU.add)

        for qt in range(NT):
            Q = Qa[qt % 2]
            P.dma("sp", Q[0:64, :, :], T.qT[:, :, qt * 128:(qt + 1) * 128].rearrange("h d t -> d h t"))
            P.dma("pool", Q[64:71, :, :], T.qaug[:, :, qt * 128:(qt + 1) * 128])
            for g in range(4):
                rhsQ = Q[:, 4 * g:4 * g + 4, :]
                for ct in range(2):
                    off = 128 * qt - 2048 * ct - 31 + 2079
                    t_ = scores(ckTa[:, g, ct * 128:(ct + 1) * 128], rhsQ, [(C.identb[:], bcast4(cmpm[:, off:off + 128]))])
                    for r in range(4):
                        P.mm(acc[r][:, 0:129], t_[:, r * 128:(r + 1) * 128], cva[:, ct, g, :], start=(ct == 0), stop=(ct == 1))
                finalize(g, qt, 0, 129, True)
                P.tt("dve", impm[:], imp[:], mulm[:, qt, :], ALU.mult)
                P.tt("dve", impm[:], impm[:], addm[:, qt, :], ALU.add)
                P.add("dve", lambda e: e.max(m8[:], impm[:]), [impm[:]], [m8[:]])
                P.add("dve", lambda e: e.match_replace(impw[:], m8[:], impm[:], -1e9), [impm[:], m8[:]], [impw[:]])
                P.add("dve", lambda e: e.max(m8b[:], impw[:]), [impw[:]], [m8b[:]])
                P.ts("dve", sel[:], impm[:], m8b[:, 7:8], ALU.is_ge)
                P.ts("dve", mneg[:], sel[:], 1.0, ALU.subtract, BIG, ALU.mult)
                P.transpose(ptr[0:64, 0:128], mneg[:], C.identb[:])
                for r in range(4):
                    P.copy("act" if r % 2 == 0 else "dve", MT[:, r, :], ptr[0:64, 0:128])
                MTf = MT[:].rearrange("p r q -> p (r q)")
                for kt in range(qt + 1):
                    extra = [(Etab[:, kt, :], MTf)]
                    if kt == qt:
                        extra.append((C.identb[:], bcast4(cz[:])))
                    t_ = scores(ksTa[:, g, kt * 128:(kt + 1) * 128], rhsQ, extra)
                    for r in range(4):
                        P.mm(acc[r][:, 0:65], t_[:, r * 128:(r + 1) * 128], vsa[:, kt, g, :], start=(kt == 0), stop=(kt == qt))
                finalize(g, qt, 1, 65, False)
                k0 = max(0, qt - 4)
                for kt in range(k0, qt + 1):
                    extra = []
                    if kt == qt - 4:
                        extra.append((C.identb[:], bcast4(wz[:])))
                    if kt == qt:
                        extra.append((C.identb[:], bcast4(cz[:])))
                    t_ = scores(kwTa[:, g, kt * 128:(kt + 1) * 128], rhsQ, extra)
                    for r in range(4):
                        P.mm(acc[r][:, 0:65], t_[:, r * 128:(r + 1) * 128], vwa[:, kt, g, :], start=(kt == k0), stop=(kt == qt))
                finalize(g, qt, 2, 65, False)
            P.copy("act", ob[:], oacc[:].rearrange("p h d -> p (h d)"))
            for k in range(8):
                P.transpose(ptr[:, k * 128:(k + 1) * 128], ob[:, k * 128:(k + 1) * 128], C.identb[:])
            P.copy("dve", oTs[:].rearrange("p k t -> p (k t)"), ptr[:])
            P.dma("sp", T.yT[:, qt * 128:(qt + 1) * 128].rearrange("(k p) t -> p k t", p=128), oTs[:])


def nsa_consts():
    bf = ml_dtypes.bfloat16
    out = {}
    h = np.arange(1, 17, dtype=np.float32)
    slopes = np.exp2(-8.0 * h / 16).astype(np.float32)
    t = np.arange(S, dtype=np.float32)

    def split3(v):
        a = v.astype(bf).astype(np.float32)
        b = (v - a).astype(bf).astype(np.float32)
        c = (v - a - b).astype(bf).astype(np.float32)
        return a, b, c

    qaug = np.zeros((7, 16, S), np.float32)
    for hh in range(16):
        sh = slopes[hh]
        s_hi = np.float32(sh).astype(bf).astype(np.float32)
        s_lo = np.float32(sh - s_hi).astype(bf).astype(np.float32)
        m = (-(np.float32(sh) * t)).astype(np.float32)
        a, b, c = split3(m)
        qaug[0, hh] = s_hi
        qaug[1, hh] = s_hi
        qaug[2, hh] = s_lo
        qaug[3, hh] = s_lo
        qaug[4, hh], qaug[5, hh], qaug[6, hh] = a, b, c
    out["qaug"] = qaug.astype(bf)

    def kaug_for(pos):
        a = (pos // 128) * 128
        i = pos % 128
        k = np.stack([a, i, a, i, np.ones_like(pos), np.ones_like(pos), np.ones_like(pos)]).astype(np.float32)
        return k.astype(bf)
    out["kaug"] = kaug_for(np.arange(S))
    out["kaug_cmp"] = kaug_for(16 * np.arange(256) + 31)
    E = np.zeros((64, NT, 128), np.float32)
    for kt in range(NT):
        E[2 * kt, kt, 0:64] = 1.0
        E[2 * kt + 1, kt, 64:128] = 1.0
    out["Etab"] = E.astype(bf)
    i = np.arange(128)
    jj = np.arange(6144) - 2079
    out["cmpmask"] = np.where(16 * i[:, None] <= jj[None, :], 0.0, -BIG).astype(bf)
    out["cz"] = np.where(i[:, None] <= i[None, :], 0.0, -BIG).astype(bf)
    out["wz"] = np.where(i[None, :] < i[:, None], 0.0, -BIG).astype(bf)
    tt_ = (np.arange(NT)[None, :, None] * 128 + np.arange(128)[:, None, None])
    n = np.arange(64)[None, None, :]
    cur = tt_ // 64
    forced = (n == 0) | (n == cur) | (n == cur - 1)
    causal = n * 64 <= tt_
    out["mulm"] = (causal & ~forced).astype(np.float32)
    out["addm"] = np.where(forced, 1e4, np.where(causal, 0.0, -1.0)).astype(np.float32)
    c = np.arange(256)
    c_lo = 16 * c
    s_lo = 64 * np.arange(64)
    ov = np.clip(np.minimum(c_lo[:, None] + 32, s_lo[None, :] + 64) - np.maximum(c_lo[:, None], s_lo[None, :]), 0, None).astype(np.float32) / 32
    ov[255] = 0.0
    out["ovl"] = np.ascontiguousarray(ov.reshape(2, 128, 64).transpose(1, 0, 2)).astype(bf)
    return out
```
